# Optimizing a Trainium2 kernel written in Bass

```python
import math
import jax, jax.numpy as jnp
from jax import lax
import numpy as np

D_MODEL = 2048
BATCH = 2
SEQ = 8192
DEPTH = 1

GRID_W = 64
CTX_LEN = 256

GLA_HEADS = 4
GLA_DK = 128
GLA_DV = 256
GLA_KEY_WIDTH = GLA_HEADS * GLA_DK
GLA_VAL_WIDTH = GLA_HEADS * GLA_DV
GATE_RANK = 16
GATE_TAU = 16.0
CHUNK = 64

FOURIER_GROUPS = 4
FOURIER_GROUP_DIM = 256
FOURIER_WIDTH = FOURIER_GROUPS * FOURIER_GROUP_DIM

N_BRANCHES = 2
D_FF = 4 * D_MODEL
N_MOD = 6
EPS = 1e-6
POS_TEMP = 10000.0

IN_SPLIT_WIDTHS = (GLA_KEY_WIDTH, GLA_KEY_WIDTH, GLA_VAL_WIDTH, GLA_VAL_WIDTH,
                   GATE_RANK, GATE_RANK, FOURIER_WIDTH, N_BRANCHES * D_MODEL)
IN_WIDTH = (2 * GLA_KEY_WIDTH + 2 * GLA_VAL_WIDTH + 2 * GATE_RANK
            + FOURIER_WIDTH + N_BRANCHES * D_MODEL)

kernel_name = "hybrid_fnet_gla_dit_block"


def rmsnorm(x, g):
    xf = x.astype(jnp.float32)
    y = xf * lax.rsqrt(jnp.mean(jnp.square(xf), axis=-1, keepdims=True) + EPS)
    return (y * g.astype(jnp.float32)).astype(x.dtype)


def modulate(h, shift, scale):
    return h * (1.0 + scale) + shift


def pos_embed_2d(n_tokens, dtype):
    rows = n_tokens // GRID_W
    row = jnp.repeat(jnp.arange(rows, dtype=jnp.float32), GRID_W)
    col = jnp.tile(jnp.arange(GRID_W, dtype=jnp.float32), rows)
    quarter = D_MODEL // 4
    omega = 1.0 / (POS_TEMP ** (jnp.arange(quarter, dtype=jnp.float32) / quarter))
    er = row[:, None] * omega[None, :]
    ec = col[:, None] * omega[None, :]
    return jnp.concatenate([jnp.sin(er), jnp.cos(er), jnp.sin(ec), jnp.cos(ec)], axis=-1).astype(dtype)


def split_proj(proj):
    parts = []
    start = 0
    for w in IN_SPLIT_WIDTHS:
        parts.append(proj[..., start:start + w])
        start += w
    return parts


def gla_inputs(parts, w_lr_f, b_lr_f, w_lr_b, b_lr_b):
    q, k, v, _, lr_f, lr_b, _, _ = parts
    b, n, _ = q.shape
    heads = lambda a, d: a.astype(jnp.float32).reshape(b, n, GLA_HEADS, d)
    q = heads(q, GLA_DK) * (GLA_DK ** -0.5)
    k = heads(k, GLA_DK)
    v = heads(v, GLA_DV)
    la_f = jax.nn.log_sigmoid((lr_f @ w_lr_f + b_lr_f).astype(jnp.float32)) / GATE_TAU
    la_b = jax.nn.log_sigmoid((lr_b @ w_lr_b + b_lr_b).astype(jnp.float32)) / GATE_TAU
    return q, k, v, heads(la_f, GLA_DK), heads(la_b, GLA_DK)


def gla_chunked(q, k, v, log_a, s0):
    b, t, h, dk = q.shape
    dv = v.shape[-1]
    n = t // CHUNK
    chunks = lambda a: a.reshape(b, n, CHUNK, h, a.shape[-1])
    q, k, v, log_a = chunks(q), chunks(k), chunks(v), chunks(log_a)
    cum = jnp.cumsum(log_a, axis=2)
    cum_last = cum[:, :, -1:]
    q_dec = q * jnp.exp(cum)
    k_inv = k * jnp.exp(-cum)
    k_end = k * jnp.exp(cum_last - cum)
    causal_in_scan = jnp.tril(jnp.ones((CHUNK, CHUNK), dtype=bool))
    scores = jnp.einsum('bnthd,bnshd->bnhts', q_dec, k_inv)
    scores = jnp.where(causal_in_scan, scores, 0.0)
    o_intra = jnp.einsum('bnhts,bnshv->bnthv', scores, v)

    def step(s, xs):
        qd, ke, vc, dec = xs
        o = jnp.einsum('bthd,bhdv->bthv', qd, s)
        s = dec[..., None] * s + jnp.einsum('bshd,bshv->bhdv', ke, vc)
        return s, o

    xs = (q_dec.swapaxes(0, 1), k_end.swapaxes(0, 1), v.swapaxes(0, 1),
          jnp.exp(cum_last[:, :, 0]).swapaxes(0, 1))
    s_final, o_inter = lax.scan(step, s0, xs)
    o = o_intra + o_inter.swapaxes(0, 1)
    return o.reshape(b, t, h, dv), s_final


def bidir_gla(q, k, v, la_f, la_b, s_f0, s_b0):
    o_f, s_f = gla_chunked(q, k, v, la_f, s_f0)
    flip = lambda a: jnp.flip(a, axis=1)
    o_b, s_b = gla_chunked(flip(q), flip(k), flip(v), flip(la_b), s_b0)
    return o_f + flip(o_b), s_f, s_b


def fourier_mix(u):
    b, n, _ = u.shape
    ug = u.astype(jnp.float32).reshape(b, n, FOURIER_GROUPS, FOURIER_GROUP_DIM)
    y = jnp.real(jnp.fft.fft2(ug, axes=(1, 3), norm='ortho'))
    return y.reshape(b, n, FOURIER_WIDTH).astype(u.dtype)


def merge_branches(parts, o_gla, gla_norm_g, w_fourier_out, w_gla_out, w_out):
    _, _, _, g_out, _, _, u_f, gates = parts
    b, n, _ = u_f.shape
    o = rmsnorm(o_gla, gla_norm_g).reshape(b, n, GLA_VAL_WIDTH).astype(u_f.dtype)
    y_gla = (o * jax.nn.silu(g_out)) @ w_gla_out
    y_fft = fourier_mix(u_f) @ w_fourier_out
    gate_fft, gate_gla = jnp.split(jax.nn.sigmoid(gates), N_BRANCHES, axis=-1)
    return (gate_fft * y_fft + gate_gla * y_gla) @ w_out


def sq_relu_mlp(h, w_in, w_out):
    return jnp.square(jax.nn.relu(h @ w_in)) @ w_out


def setup_inputs(seed: int = 0) -> dict:
    key = jax.random.key(seed)
    ks = jax.random.split(key, 24)
    nrm = lambda k, shape, s: jax.random.normal(k, shape, jnp.float32) * s
    L = DEPTH
    return {
        "x": nrm(ks[0], (BATCH, SEQ, D_MODEL), 1.0),
        "c": nrm(ks[1], (BATCH, D_MODEL), 1.0),
        "ctx": nrm(ks[2], (BATCH, CTX_LEN, D_MODEL), 1.0),
        "c_ctx": nrm(ks[3], (D_MODEL,), 1.0),
        "w_mod": nrm(ks[4], (L, D_MODEL, N_MOD * D_MODEL), 0.5 * D_MODEL ** -0.5),
        "b_mod": nrm(ks[5], (L, N_MOD * D_MODEL), 0.02),
        "norm1_g": 1.0 + nrm(ks[6], (L, D_MODEL), 0.05),
        "norm2_g": 1.0 + nrm(ks[7], (L, D_MODEL), 0.05),
        "w_in": nrm(ks[8], (L, D_MODEL, IN_WIDTH), D_MODEL ** -0.5),
        "w_lr_f": nrm(ks[9], (L, GATE_RANK, GLA_KEY_WIDTH), GATE_RANK ** -0.5),
        "b_lr_f": nrm(ks[10], (L, GLA_KEY_WIDTH), 0.1),
        "w_lr_b": nrm(ks[11], (L, GATE_RANK, GLA_KEY_WIDTH), GATE_RANK ** -0.5),
        "b_lr_b": nrm(ks[12], (L, GLA_KEY_WIDTH), 0.1),
        "gla_norm_g": 1.0 + nrm(ks[13], (L, GLA_DV), 0.05),
        "w_fourier_out": nrm(ks[14], (L, FOURIER_WIDTH, D_MODEL), FOURIER_WIDTH ** -0.5),
        "w_gla_out": nrm(ks[15], (L, GLA_VAL_WIDTH, D_MODEL), GLA_VAL_WIDTH ** -0.5),
        "w_out": nrm(ks[16], (L, D_MODEL, D_MODEL), D_MODEL ** -0.5),
        "w_mlp_in": nrm(ks[17], (L, D_MODEL, D_FF), D_MODEL ** -0.5),
        "w_mlp_out": nrm(ks[18], (L, D_FF, D_MODEL), D_FF ** -0.5),
        "final_norm_g": 1.0 + nrm(ks[19], (D_MODEL,), 0.05),
    }


def reference(x, c, ctx, c_ctx, w_mod, b_mod, norm1_g, norm2_g, w_in, w_lr_f, b_lr_f,
              w_lr_b, b_lr_b, gla_norm_g, w_fourier_out, w_gla_out, w_out, w_mlp_in,
              w_mlp_out, final_norm_g):
    b = x.shape[0]
    x = x + pos_embed_2d(x.shape[1], x.dtype)[None]
    x_ctx = ctx
    for i in range(DEPTH):
        last = i == DEPTH - 1
        mod_lat = jax.nn.silu(c) @ w_mod[i] + b_mod[i]
        mod_ctx = (jax.nn.silu(c_ctx) @ w_mod[i] + b_mod[i])[None]
        sh1, sc1, gt1, sh2, sc2, gt2 = [m[:, None] for m in jnp.split(mod_lat, N_MOD, axis=-1)]
        csh1, csc1, cgt1, csh2, csc2, cgt2 = [m[:, None] for m in jnp.split(mod_ctx, N_MOD, axis=-1)]

        h_ctx = modulate(rmsnorm(x_ctx, norm1_g[i]), csh1, csc1)
        parts_ctx = split_proj(h_ctx @ w_in[i])
        q_c, k_c, v_c, laf_c, lab_c = gla_inputs(parts_ctx, w_lr_f[i], b_lr_f[i], w_lr_b[i], b_lr_b[i])
        s_zero = jnp.zeros((b, GLA_HEADS, GLA_DK, GLA_DV), jnp.float32)
        o_ctx, s_f_ctx, s_b_ctx = bidir_gla(q_c, k_c, v_c, laf_c, lab_c, s_zero, s_zero)

        h = modulate(rmsnorm(x, norm1_g[i]), sh1, sc1)
        parts = split_proj(h @ w_in[i])
        q, k, v, la_f, la_b = gla_inputs(parts, w_lr_f[i], b_lr_f[i], w_lr_b[i], b_lr_b[i])
        o_lat, _, _ = bidir_gla(q, k, v, la_f, la_b, s_f_ctx, s_b_ctx)
        y = merge_branches(parts, o_lat, gla_norm_g[i], w_fourier_out[i], w_gla_out[i], w_out[i])
        x = x + gt1 * y

        h2 = modulate(rmsnorm(x, norm2_g[i]), sh2, sc2)
        x = x + gt2 * sq_relu_mlp(h2, w_mlp_in[i], w_mlp_out[i])

        if not last:
            y_ctx = merge_branches(parts_ctx, o_ctx, gla_norm_g[i], w_fourier_out[i], w_gla_out[i], w_out[i])
            x_ctx = x_ctx + cgt1 * y_ctx
            h2_ctx = modulate(rmsnorm(x_ctx, norm2_g[i]), csh2, csc2)
            x_ctx = x_ctx + cgt2 * sq_relu_mlp(h2_ctx, w_mlp_in[i], w_mlp_out[i])

    return rmsnorm(x, final_norm_g)
```

```python
import contextlib
import numpy as np
import ml_dtypes
import concourse.bass as bass
import concourse.mybir as mybir
from concourse.bass_utils import run_bass_kernel_spmd

F32 = mybir.dt.float32
BF16 = mybir.dt.bfloat16
AF = mybir.ActivationFunctionType
ALU = mybir.AluOpType
NPBF = ml_dtypes.bfloat16

D = 2048
T = 8192
NT = 64
NCT = 2
DFF = 8192
EPS = 1e-6


ALL_BUFS = []


class Buf:
    __slots__ = ("last_w", "readers", "excl")

    def __init__(self, excl=False):
        self.last_w = None
        self.readers = []
        self.excl = excl
        ALL_BUFS.append(self)


class Op:
    __slots__ = ("eng", "fn", "deps", "needed", "is_dma", "sem", "val", "idx")

    def __init__(self, eng, fn, is_dma):
        self.eng = eng
        self.fn = fn
        self.deps = []
        self.needed = False
        self.is_dma = is_dma
        self.sem = None
        self.val = None
        self.idx = 0


ENGS = ("pe", "dve", "act", "pool", "sp")


class Prog:
    def __init__(self, nc, n_dma_sems=24):
        self.nc = nc
        self.ops = {e: [] for e in ENGS}
        self.all_ops = []
        self.n_dma_sems = n_dma_sems
        self.dma_hist = {e: [] for e in ENGS}
        self.limit = None
        self.on_limit = None
        self.cc_ops = []

    def _add(self, eng, fn, reads, writes, is_dma=False):
        op = Op(eng, fn, is_dma)
        deps = []
        for b in reads:
            if b.last_w is not None:
                deps.append(b.last_w)
            if b.excl:
                deps.extend(r for r in b.readers if r.eng != eng)
        for b in writes:
            if b.last_w is not None:
                deps.append(b.last_w)
            deps.extend(b.readers)
        for b in reads:
            b.readers.append(op)
        for b in writes:
            b.last_w = op
            b.readers = []
        seen = set()
        for d in deps:
            if d is op or id(d) in seen:
                continue
            seen.add(id(d))
            if (not d.is_dma) and d.eng == eng and eng == "pe":
                continue
            op.deps.append(d)
            d.needed = True
        if is_dma:
            hist = self.dma_hist[eng]
            i = len(hist)
            if i >= self.n_dma_sems:
                op.deps.append(hist[i - self.n_dma_sems])
            hist.append(op)
            op.needed = True
            op.idx = i
        self.ops[eng].append(op)
        self.all_ops.append(op)
        if self.limit is not None and len(self.all_ops) >= self.limit:
            self.limit = None
            self.on_limit()
        return op

    def op(self, eng, fn, reads=(), writes=()):
        return self._add(eng, fn, reads, writes)

    def cc(self, fn, reads=(), writes=()):
        op = self._add("pool", fn, reads, writes)
        op.needed = True
        op.is_dma = True
        op.idx = -1 - len(self.cc_ops)
        self.cc_ops.append(op)
        return op

    def dma(self, eng, fn, reads=(), writes=()):
        return self._add(eng, fn, reads, writes, is_dma=True)

    def emit(self, final_waits=()):
        nc = self.nc
        with contextlib.ExitStack() as st:
            esem = {e: st.enter_context(nc.semaphore(f"s_{e}")) for e in ENGS}
            dsem = {e: [st.enter_context(nc.semaphore(f"d_{e}_{i}")) for i in range(self.n_dma_sems)]
                    for e in ENGS if self.dma_hist[e]}
            cnt = {e: 0 for e in ENGS}
            ccsem = [st.enter_context(nc.semaphore(f"cc_{i}")) for i in range(len(self.cc_ops))]
            for i, op in enumerate(self.cc_ops):
                op.sem = ccsem[i]
                op.val = 1
            for op in self.all_ops:
                if op.idx < 0:
                    continue
                if op.is_dma:
                    op.sem = dsem[op.eng][op.idx % self.n_dma_sems]
                    op.val = 16 * (op.idx // self.n_dma_sems + 1)
                elif op.needed:
                    cnt[op.eng] += 1
                    op.sem = esem[op.eng]
                    op.val = cnt[op.eng]
            block = st.enter_context(nc.Block())

            def make(e):
                def body(eng):
                    known = {}

                    def wait(d):
                        k = id(d.sem)
                        if known.get(k, 0) < d.val:
                            eng.wait_ge(d.sem, d.val)
                            known[k] = d.val

                    for op in self.ops[e]:
                        for d in op.deps:
                            wait(d)
                        ins = op.fn(eng)
                        if op.idx < 0:
                            ins.then_inc(op.sem, 1)
                        elif op.is_dma:
                            ins.then_inc(op.sem, 16)
                        elif op.needed:
                            ins.then_inc(op.sem, 1)
                    if e == "sp":
                        for d in final_waits:
                            wait(d)
                return body

            block.tensor(make("pe"))
            block.vector(make("dve"))
            block.scalar(make("act"))
            block.gpsimd(make("pool"))
            block.sync(make("sp"))


_CONSTS = None


def _consts():
    global _CONSTS
    if _CONSTS is not None:
        return _CONSTS
    c = {}
    quarter = D // 4
    omega = (1.0 / (np.float32(10000.0) ** (np.arange(quarter, dtype=np.float32) / np.float32(quarter)))).astype(np.float32)
    rows = np.arange(128, dtype=np.float32)
    cols = np.arange(64, dtype=np.float32)
    er = (rows[:, None] * omega[None, :]).astype(np.float32)
    ec = (cols[:, None] * omega[None, :]).astype(np.float32)
    Rt = np.concatenate([np.sin(er), np.cos(er)], axis=-1).astype(np.float32)
    Ct = np.concatenate([np.sin(ec), np.cos(ec)], axis=-1).astype(np.float32)
    c["PR"] = np.ascontiguousarray(np.repeat(Rt.reshape(64, 2, 1, 1024), 64, axis=2).reshape(64, 128, 1024))
    c["PC"] = np.ascontiguousarray(np.concatenate([Ct, Ct], axis=0))
    ch = np.arange(256, dtype=np.float64)
    ang = 2 * np.pi * np.outer(ch, ch) / 256.0
    Fc = np.concatenate([np.cos(ang), -np.sin(ang)], axis=1) / 16.0
    c["Fc"] = np.ascontiguousarray(Fc.reshape(2, 128, 512).transpose(1, 0, 2)).astype(NPBF)
    pp = np.arange(128, dtype=np.float64)[:, None, None]
    aa = np.arange(64, dtype=np.float64)[None, :, None]
    kp = np.arange(128, dtype=np.float64)[None, None, :]
    ang1 = 2 * np.pi * ((64 * pp + aa) * kp % 8192) / 8192.0
    c["T1c"] = (np.cos(ang1) / np.sqrt(128.0)).astype(NPBF)
    c["T1s"] = (np.sin(ang1) / np.sqrt(128.0)).astype(NPBF)
    a2 = np.arange(64, dtype=np.float64)
    ang2 = 2 * np.pi * np.outer(a2, a2) / 64.0
    c["C2S2"] = np.ascontiguousarray(np.stack([np.cos(ang2) / 8.0, np.sin(ang2) / 8.0], axis=1)).astype(NPBF)
    r = np.arange(128)[:, None]
    t = np.arange(128)[None, :]
    s16 = np.float32(-1.0 / 16.0)
    tri = np.stack([(r <= t), (r >= t), (r > t), (r < t)], axis=1).astype(np.float32) * s16
    c["tri"] = np.ascontiguousarray(tri)
    c["mask2"] = np.ascontiguousarray(np.concatenate([(r <= t), (r >= t)], axis=1).astype(np.float32))
    c["identb"] = np.eye(128, dtype=np.float32).astype(NPBF)
    c["identf"] = np.eye(128, dtype=np.float32)
    _CONSTS = c
    return c


class _Stop(Exception):
    pass


def build_nc(stop=None):
    nc = bass.Bass("TRN2", target_bir_lowering=False)
    try:
        _build(nc, stop)
    except _Stop:
        pass
    return nc


def _build(nc, stop):
    st = contextlib.ExitStack()

    def din(name, shape, dt=F32):
        return nc.dram_tensor(name, list(shape), dt, kind="ExternalInput").ap()

    def dtmp(name, shape, dt):
        return nc.dram_tensor(name, list(shape), dt).ap()

    def sb(name, shape, dt):
        return st.enter_context(nc.sbuf_tensor("s_" + name, list(shape), dt))

    x_b = din("x_b", [T, D]); ctx_b = din("ctx_b", [256, D]); x_own = din("x_own", [2048, D])
    cvecT = din("cvecT", [D, 2]); w_mod = din("w_mod", [D, 6 * D]); b_mod2 = din("b_mod2", [2, 6 * D])
    gvecs = din("gvecs", [128, 3, 16]); ggrow = din("ggrow", [1, 256])
    wA_d = din("wA", [D, 1024]); wLR_d = din("wLR", [D, 32]); wG_d = din("wG", [D, 4096]); wlr_d = din("wlr", [33, 256])
    wfo_d = din("w_fo", [1024, D]); wgo_d = din("w_go", [1024, D]); wo_d = din("w_o", [D, D])
    w1_d = din("w1", [D, DFF]); w2_d = din("w2", [DFF, D])
    PR_d = din("PR", [64, 128, 1024]); PRo_d = din("PRown", [16, 128, 1024]); PC_d = din("PC", [128, 1024])
    Fc_d = din("Fc", [128, 2, 512], BF16); T1c_d = din("T1c", [128, 64, 128], BF16); T1s_d = din("T1s", [128, 64, 128], BF16)
    C2S2_d = din("C2S2", [64, 2, 64], BF16); tri_d = din("tri", [128, 4, 128]); mask2_d = din("mask2", [128, 256])
    identb_d = din("identb", [128, 128], BF16); identf_d = din("identf", [128, 128]); sel_d = din("sel", [128, 8])
    out_d = nc.dram_tensor("out", [2048, D], F32, kind="ExternalOutput").ap()

    sgD = dtmp("sgD", [T, 256], BF16); uD = dtmp("uD", [T, 256], BF16); oiD = dtmp("oiD", [T, 256], F32)
    UD = dtmp("UD", [NT + NCT, 128, 512], F32); GdD = dtmp("GdD", [128, 64, 512], BF16)
    srcA = dtmp("srcA", [256, T], BF16); srcF = dtmp("srcF", [256, T], BF16)
    gatA = dtmp("gatA", [2048, T], BF16); gatF = dtmp("gatF", [2048, T], BF16)

    R1 = sb("R1", [128, 16384], BF16); R2 = sb("R2", [128, 16384], BF16); R3 = sb("R3", [128, 16384], BF16)
    RF = sb("RF", [128, 32768], BF16)
    RFf = RF[:, :].bitcast(F32)
    bR1a, bR1b, bR2a, bR2b, bR3a, bR3b, bRFa, bRFb, bRFc, bRFd = [Buf() for _ in range(10)]

    identb = sb("identb", [128, 128], BF16); identf = sb("identf", [128, 128], F32)
    tri = sb("tri", [128, 4, 128], F32); mask2 = sb("mask2", [128, 256], F32)
    Fc = sb("Fc", [128, 2, 512], BF16); C2S2 = sb("C2S2", [64, 2, 64], BF16); PC = sb("PC", [128, 1024], F32)
    sel = sb("sel", [128, 8], F32); gv = sb("gv", [128, 3, 16], F32); ggb = sb("ggb", [128, 256], F32)
    ones1 = sb("ones1", [1, 128], F32); ggr = sb("ggr", [1, 256], F32)
    wlr = sb("wlr", [33, 256], F32); wLR = sb("wLR", [128, 16, 32], BF16)
    cT = sb("cT", [128, 16, 2], F32); scT = sb("scT", [128, 16, 2], BF16)
    mrow = sb("mrow", [2, 2, 512], F32); bmod = sb("bmod", [2, 2, 512], F32); b_mrow = [Buf(), Buf()]; b_bm = [Buf(), Buf()]
    modT = sb("modT", [128, 96, 2], F32)
    ps = sb("pscal", [128, 9, 16], F32)
    b_const = Buf(); b_mod = Buf(); b_ps = Buf()

    pT = st.enter_context(nc.psum_tensor("pT", [128, 2048], BF16))
    pM = [st.enter_context(nc.psum_tensor(f"pM{i}", [128, 512], F32)) for i in range(4)]
    pS = [st.enter_context(nc.psum_tensor(f"pS{i}", [128, 512], F32)) for i in range(2)]
    b_pT = Buf(True); b_pM = [Buf(True) for _ in range(4)]; b_pS = [Buf(True) for _ in range(2)]

    P = Prog(nc)

    import os
    if os.environ.get("STOPN"):
        P.limit = int(os.environ["STOPN"])

        def _on_limit():
            P.emit(final_waits=[o for o in P.all_ops if o.is_dma])
            st.close()
            raise _Stop()
        P.on_limit = _on_limit

    def dbg_out(items):
        ds = []
        for (name, ap, shape, dt, bufs) in items:
            o = nc.dram_tensor("dbg_" + name, list(shape), dt, kind="ExternalOutput").ap()
            ds.append(P.dma("sp", lambda e, o=o, ap=ap: e.dma_start(out=o, in_=ap), reads=list(bufs)))
        P.emit(final_waits=ds)
        st.close()
        raise _Stop()

    for dst, src in ((identb, identb_d), (identf, identf_d), (tri, tri_d), (mask2, mask2_d), (Fc, Fc_d), (C2S2, C2S2_d),
                     (PC, PC_d), (sel, sel_d), (gv, gvecs), (ggr, ggrow), (wlr, wlr_d)):
        P.dma("sp", lambda e, dst=dst, src=src: e.dma_start(out=dst[:], in_=src), writes=[b_const])
    P.dma("sp", lambda e: e.dma_start(out=cT[:], in_=cvecT.rearrange("(k p) r -> p k r", p=128)), writes=[b_const])
    P.dma("pool", lambda e: e.dma_start(out=wLR[:], in_=wLR_d.rearrange("(k p) n -> p k n", p=128)), writes=[b_const])
    P.op("dve", lambda e: e.memset(ones1[:], 1.0), writes=[b_const])
    P.op("pe", lambda e: e.matmul(pS[0][:, 0:256], lhsT=ones1[:], rhs=ggr[:], start=True, stop=True), reads=[b_const], writes=[b_pS[0]])
    P.op("dve", lambda e: e.tensor_copy(out=ggb[:], in_=pS[0][:, 0:256]), reads=[b_pS[0]], writes=[b_const])

    P.op("act", lambda e: e.activation(out=scT[:], in_=cT[:], func=AF.Silu), reads=[b_const], writes=[b_mod])
    wm_v = w_mod.rearrange("(k p) n -> p k n", p=128)
    wmb = [RF[:, 0:8192].rearrange("p (k n) -> p k n", k=16), RF[:, 8192:16384].rearrange("p (k n) -> p k n", k=16)]
    bwm = [bRFa, bRFb]
    for n in range(24):
        i = n % 2
        P.dma("pool", lambda e, n=n, i=i: e.dma_start(out=wmb[i], in_=wm_v[:, :, n * 512:(n + 1) * 512]), writes=[bwm[i]])
        for k in range(16):
            P.op("pe", lambda e, k=k, i=i, n=n: e.matmul(pM[n % 4][0:2, :], lhsT=scT[:, k, :], rhs=wmb[i][:, k, :],
                                                        start=(k == 0), stop=(k == 15)),
                 reads=[b_mod, bwm[i]], writes=[b_pM[n % 4]])
        P.dma("sp", lambda e, n=n, i=i: e.dma_start(out=bmod[:, i, :], in_=b_mod2[:, n * 512:(n + 1) * 512]), writes=[b_bm[i]])
        P.op("dve", lambda e, n=n, i=i: e.tensor_tensor(out=mrow[:, i, :], in0=pM[n % 4][0:2, :], in1=bmod[:, i, :], op=ALU.add),
             reads=[b_pM[n % 4], b_bm[i]], writes=[b_mrow[i]])
        for q in range(4):
            blk = 4 * n + q
            P.op("pe", lambda e, blk=blk, q=q, i=i: e.matmul(pS[0][:, 2 * blk:2 * blk + 2], lhsT=mrow[:, i, q * 128:(q + 1) * 128],
                                                            rhs=identf[0:2, 0:2], start=True, stop=True),
                 reads=[b_mrow[i], b_const], writes=[b_pS[0]])
    P.op("dve", lambda e: e.tensor_copy(out=modT[:].rearrange("p a b -> p (a b)"), in_=pS[0][:, 0:192]), reads=[b_pS[0]], writes=[b_mod])

    def mcol(v, r):
        return modT[:, v * 16:(v + 1) * 16, r]

    def pscal(i):
        return ps[:, i, :]
    for (i, gi, v, r) in ((0, 0, 1, 0), (2, 0, 1, 1), (5, 1, 4, 0)):
        P.op("dve", lambda e, i=i, gi=gi, v=v, r=r: e.scalar_tensor_tensor(out=pscal(i), in0=mcol(v, r), scalar=1.0, in1=gv[:, gi, :],
                                                                          op0=ALU.add, op1=ALU.mult), reads=[b_mod, b_const], writes=[b_ps])
    for (i, v, r) in ((1, 0, 0), (3, 0, 1), (4, 2, 0), (6, 3, 0), (7, 5, 0)):
        P.op("dve", lambda e, i=i, v=v, r=r: e.tensor_copy(out=pscal(i), in_=mcol(v, r)), reads=[b_mod], writes=[b_ps])
    P.op("dve", lambda e: e.tensor_copy(out=pscal(8), in_=gv[:, 2, :]), reads=[b_const], writes=[b_ps])


    SC = sb("SC", [128, 10240], BF16)

    class Arena:
        def __init__(self):
            self.off = 0

        def reset(self):
            self.off = 0

        def alloc(self, name, shape, dt):
            n = int(np.prod(shape[1:])) * (2 if dt == F32 else 1)
            n = (n + 1) // 2 * 2
            v = SC[0:shape[0], self.off:self.off + n]
            self.off += n
            assert self.off <= 10240, (name, self.off)
            if dt == F32:
                v = v.bitcast(F32)
            if len(shape) == 3:
                v = v.rearrange("p (a b) -> p a b", a=shape[1])
            elif len(shape) == 4:
                v = v.rearrange("p (a b c) -> p a b c", a=shape[1], b=shape[2])
            return v
    ar = Arena()
    sbt = ar.alloc
    bar_t = sb("bar_t", [128, 2], F32)

    def barrier():
        P.op("dve", lambda e: e.memset(bar_t[:], 0.0), writes=list(ALL_BUFS))

    if stop == "p0":
        dbg_out([("modT", modT[:], [128, 96, 2], F32, [b_mod]), ("ps", ps[:], [128, 9, 16], F32, [b_ps]), ("ggb", ggb[:], [128, 256], F32, [b_const])])
    wA = R2[:, :].rearrange("p (k n) -> p k n", k=16)
    P.dma("pool", lambda e: e.dma_start(out=wA[:, 0:8, :], in_=wA_d.rearrange("(k p) n -> p k n", p=128)[:, 0:8, :]), writes=[bR2a, bR2b])
    P.dma("pool", lambda e: e.dma_start(out=wA[:, 8:16, :], in_=wA_d.rearrange("(k p) n -> p k n", p=128)[:, 8:16, :]), writes=[bR2a, bR2b])
    qdT = R1[:, :].rearrange("p (d n t) -> p d n t", d=2, n=NT)
    NXB = 2
    xt = [RFf[:, 8192 + i * 2048: 8192 + (i + 1) * 2048] for i in range(NXB)]; b_xt = [Buf() for _ in range(NXB)]
    prt = [RFf[:, 12288 + i * 1024: 12288 + (i + 1) * 1024] for i in range(2)]; b_prt = [Buf() for _ in range(2)]
    xn = [R3[:, i * 2048:(i + 1) * 2048] for i in range(2)]; b_xn = [Buf() for _ in range(2)]
    hT = [R3[:, 4096 + i * 2048: 4096 + (i + 1) * 2048].rearrange("p (k t) -> p k t", k=16) for i in range(2)]; b_hT = [Buf() for _ in range(2)]
    junk = R3[:, 8192:10240]; b_junk = Buf()
    ssq = sb("ssq", [128, 4], F32); b_ssq = Buf()
    qk2 = sbt("qk", [128, 2, 256], F32); b_qk2 = [Buf(), Buf()]
    qkT = sbt("qkT", [128, 256], F32); b_qkT = Buf()
    vt2 = sbt("vt", [128, 2, 256], BF16); b_vt2 = [Buf(), Buf()]
    sgt = sbt("sgt", [128, 2, 256], BF16); b_sg = [Buf(), Buf()]
    ut = sbt("ut", [128, 2, 256], BF16); b_ut = [Buf(), Buf()]
    lrT2 = sbt("lrT", [33, 2, 128], F32); b_lrT2 = [Buf(), Buf()]
    ll = sbt("ll", [128, 256], F32); b_ll = Buf()
    ee = sbt("ee", [128, 3, 256], F32); b_ee = Buf()
    kinvT = sbt("kinvT", [128, 256], BF16); b_kinv = Buf()
    kend = sbt("kend", [128, 256], BF16); b_kend = Buf()
    scm = sbt("scm", [128, 256], BF16); b_scm = Buf()
    oi = sbt("oi", [128, 2, 256], F32); b_oi = [Buf(), Buf()]
    Ut = sbt("Ut", [128, 2, 512], F32); b_Ut = [Buf(), Buf()]
    dec = sb("dec", [128, 2, NT + NCT], F32); b_dec = Buf()
    b_qdT = Buf(); b_sgD = Buf(); b_uD = Buf(); b_oiD = Buf(); b_UD = Buf()
    P.op("dve", lambda e: e.memset(lrT2[:], 1.0), writes=b_lrT2)

    tiles = [("c", 0), ("c", 1)] + [("l", n) for n in range(NT)]

    def back_stages(ti):
        kind, n = tiles[ti]
        lat = kind == "l"
        i2 = ti % 2
        qk = qk2[:, i2, :]; b_qk = b_qk2[i2]; vt = vt2[:, i2, :]; b_vt = b_vt2[i2]; lrT = lrT2[:, i2, :]; b_lrT = b_lrT2[i2]
        ub = (0, 2)[i2]

        def s1():
            for h in range(2):
                P.op("pe", lambda e, h=h: e.transpose(out=pS[0][:, h * 128:(h + 1) * 128], in_=qk[:, h * 128:(h + 1) * 128], identity=identf[:]),
                     reads=[b_qk, b_const], writes=[b_pS[0]])
            P.op("pe", lambda e: e.matmul(pS[0][:, 256:512], lhsT=lrT, rhs=wlr[:, :], start=True, stop=True),
                 reads=[b_lrT, b_const], writes=[b_pS[0]])
            P.op("act", lambda e: e.activation(out=ll[:], in_=pS[0][:, 256:512], func=AF.Exp, scale=-1.0), reads=[b_pS[0]], writes=[b_ll])
            P.op("dve", lambda e: e.tensor_copy(out=qkT[:], in_=pS[0][:, 0:256]), reads=[b_pS[0]], writes=[b_qkT])
            P.op("act", lambda e: e.activation(out=ll[:], in_=ll[:], func=AF.Ln, bias=1.0), reads=[b_ll], writes=[b_ll])

        def s2():
            P.op("pe", lambda e: e.matmul(pS[1][:, 0:128], lhsT=ll[:, 0:128], rhs=tri[:, 0, :], start=True, stop=True), reads=[b_ll, b_const], writes=[b_pS[1]])
            P.op("pe", lambda e: e.matmul(pS[1][:, 128:256], lhsT=ll[:, 128:256], rhs=tri[:, 1, :], start=True, stop=True), reads=[b_ll, b_const], writes=[b_pS[1]])
            P.op("pe", lambda e: e.matmul(pS[1][:, 256:384], lhsT=tri[:, 2, :], rhs=ll[:, 0:128], start=True, stop=True), reads=[b_ll, b_const], writes=[b_pS[1]])
            P.op("pe", lambda e: e.matmul(pS[1][:, 384:512], lhsT=tri[:, 3, :], rhs=ll[:, 128:256], start=True, stop=True), reads=[b_ll, b_const], writes=[b_pS[1]])
            P.op("act", lambda e: e.activation(out=ee[:, 0, :], in_=pS[1][:, 0:256], func=AF.Exp), reads=[b_pS[1]], writes=[b_ee])
            P.op("act", lambda e: e.activation(out=ee[:, 1, :], in_=pS[1][:, 0:256], func=AF.Exp, scale=-1.0), reads=[b_pS[1]], writes=[b_ee])
            P.op("act", lambda e: e.activation(out=ee[:, 2, :], in_=pS[1][:, 256:512], func=AF.Exp), reads=[b_pS[1]], writes=[b_ee])
            P.op("dve", lambda e: e.tensor_copy(out=dec[:, 0, ti:ti + 1], in_=ee[:, 0, 127:128]), reads=[b_ee], writes=[b_dec])
            P.op("dve", lambda e: e.tensor_copy(out=dec[:, 1, ti:ti + 1], in_=ee[:, 0, 128:129]), reads=[b_ee], writes=[b_dec])
            for d_ in range(2):
                if lat:
                    P.op("dve", lambda e, d_=d_: e.scalar_tensor_tensor(out=qdT[:, d_, n, :], in0=qkT[:, 0:128], scalar=float(128.0 ** -0.5),
                                                                        in1=ee[:, 0, d_ * 128:(d_ + 1) * 128], op0=ALU.mult, op1=ALU.mult),
                         reads=[b_qkT, b_ee], writes=[b_qdT])
                    P.op("pool", lambda e, d_=d_: e.tensor_tensor(out=kinvT[:, d_ * 128:(d_ + 1) * 128], in0=qkT[:, 128:256],
                                                                  in1=ee[:, 1, d_ * 128:(d_ + 1) * 128], op=ALU.mult),
                         reads=[b_qkT, b_ee], writes=[b_kinv])
                P.op("dve", lambda e, d_=d_: e.tensor_tensor(out=kend[:, d_ * 128:(d_ + 1) * 128], in0=qk[:, 128:256],
                                                             in1=ee[:, 2, d_ * 128:(d_ + 1) * 128], op=ALU.mult),
                     reads=[b_qk, b_ee], writes=[b_kend])

        def s3():
            if lat:
                for d_ in range(2):
                    P.op("pe", lambda e, d_=d_: e.matmul(pS[0][:, d_ * 128:(d_ + 1) * 128], lhsT=kinvT[:, d_ * 128:(d_ + 1) * 128],
                                                         rhs=qdT[:, d_, n, :], start=True, stop=True),
                         reads=[b_kinv, b_qdT], writes=[b_pS[0]])
                P.op("dve", lambda e: e.tensor_tensor(out=scm[:], in0=pS[0][:, 0:256], in1=mask2[:], op=ALU.mult), reads=[b_pS[0], b_const], writes=[b_scm])
            for d_ in range(2):
                P.op("pe", lambda e, d_=d_: e.matmul(pM[ub][:, d_ * 256:(d_ + 1) * 256], lhsT=kend[:, d_ * 128:(d_ + 1) * 128], rhs=vt, start=True, stop=True),
                     reads=[b_kend, b_vt], writes=[b_pM[ub]])
            P.op("dve", lambda e: e.tensor_copy(out=Ut[:, i2, :], in_=pM[ub][:, :]), reads=[b_pM[ub]], writes=[b_Ut[i2]])
            P.dma("sp", lambda e: e.dma_start(out=UD[ti], in_=Ut[:, i2, :]), reads=[b_Ut[i2]], writes=[b_UD])

        def s4():
            if lat:
                for d_ in range(2):
                    P.op("pe", lambda e, d_=d_: e.matmul(pS[0][:, 256:512], lhsT=scm[:, d_ * 128:(d_ + 1) * 128], rhs=vt, start=(d_ == 0), stop=(d_ == 1)),
                         reads=[b_scm, b_vt], writes=[b_pS[0]])
                P.op("act", lambda e: e.activation(out=oi[:, i2, :], in_=pS[0][:, 256:512], func=AF.Identity), reads=[b_pS[0]], writes=[b_oi[i2]])
                P.dma("sp", lambda e: e.dma_start(out=oiD[n * 128:(n + 1) * 128, :], in_=oi[:, i2, :]), reads=[b_oi[i2]], writes=[b_oiD])
        return [s1, s2, s3, s4]

    def xload(ti):
        kind, n = tiles[ti]
        lat = kind == "l"
        xb = xt[ti % NXB]; bxb = b_xt[ti % NXB]; i2 = ti % 2
        src = x_b[n * 128:(n + 1) * 128, :] if lat else ctx_b[n * 128:(n + 1) * 128, :]
        P.dma("sp", lambda e: e.dma_start(out=xb, in_=src), writes=[bxb])
        if lat:
            P.dma("sp", lambda e: e.dma_start(out=prt[i2], in_=PR_d[n]), writes=[b_prt[i2]])

    def front(ti, hooks):
        kind, n = tiles[ti]
        lat = kind == "l"
        xb = xt[ti % NXB]; bxb = b_xt[ti % NXB]
        i2 = ti % 2
        qk = qk2[:, i2, :]; b_qk = b_qk2[i2]; vt = vt2[:, i2, :]; b_vt = b_vt2[i2]; lrT = lrT2[:, i2, :]; b_lrT = b_lrT2[i2]
        if ti + 1 < len(tiles):
            xload(ti + 1)
        if lat:
            P.op("dve", lambda e: e.tensor_tensor(out=xb[:, 0:1024], in0=xb[:, 0:1024], in1=prt[i2], op=ALU.add), reads=[b_prt[i2]], writes=[bxb])
            P.op("pool", lambda e: e.tensor_tensor(out=xb[:, 1024:2048], in0=xb[:, 1024:2048], in1=PC[:], op=ALU.add), reads=[b_const], writes=[bxb])
        P.op("act", lambda e: e.activation(out=junk, in_=xb, func=AF.Square, accum_out=ssq[:, 0:1]), reads=[bxb], writes=[b_junk, b_ssq])
        P.op("act", lambda e: e.activation(out=ssq[:, 1:2], in_=ssq[:, 0:1], func=AF.Sqrt, scale=1.0 / D, bias=EPS), reads=[b_ssq], writes=[b_ssq])
        P.op("dve", lambda e: e.reciprocal(out=ssq[:, 2:3], in_=ssq[:, 1:2]), reads=[b_ssq], writes=[b_ssq])
        P.op("act", lambda e: e.activation(out=xn[i2], in_=xb, func=AF.Copy, scale=ssq[:, 2:3]), reads=[bxb, b_ssq], writes=[b_xn[i2]])
        for k in range(16):
            P.op("pe", lambda e, k=k: e.transpose(out=pT[:, k * 128:(k + 1) * 128], in_=xn[i2][:, k * 128:(k + 1) * 128], identity=identb[:]),
                 reads=[b_xn[i2], b_const], writes=[b_pT])
        gi, si = (0, 1) if lat else (2, 3)
        for k in range(16):
            if k % 2 == 0:
                P.op("dve", lambda e, k=k: e.tensor_scalar(out=hT[i2][:, k, :], in0=pT[:, k * 128:(k + 1) * 128],
                                                           scalar1=ps[:, gi, k:k + 1], scalar2=ps[:, si, k:k + 1], op0=ALU.mult, op1=ALU.add),
                     reads=[b_pT, b_ps], writes=[b_hT[i2]])
            else:
                P.op("act", lambda e, k=k: e.activation(out=hT[i2][:, k, :], in_=pT[:, k * 128:(k + 1) * 128],
                                                        func=AF.Identity, scale=ps[:, gi, k:k + 1], bias=ps[:, si, k:k + 1]),
                     reads=[b_pT, b_ps], writes=[b_hT[i2]])
        mb = (0, 1) if i2 == 0 else (2, 3)
        lb = 3 if i2 == 0 else 1
        for k in range(16):
            for h in range(2):
                P.op("pe", lambda e, k=k, h=h: e.matmul(pM[mb[h]][:, :], lhsT=hT[i2][:, k, :], rhs=wA[:, k, h * 512:(h + 1) * 512],
                                                        start=(k == 0), stop=(k == 15)),
                     reads=[b_hT[i2], bR2a, bR2b], writes=[b_pM[mb[h]]])
            P.op("pe", lambda e, k=k: e.matmul(pM[lb][0:32, 0:128], lhsT=wLR[:, k, :], rhs=hT[i2][:, k, :], start=(k == 0), stop=(k == 15)),
                 reads=[b_hT[i2], b_const], writes=[b_pM[lb]])
            if k in hooks:
                hooks[k]()
        m0, m1 = pM[mb[0]], pM[mb[1]]
        bm0, bm1 = b_pM[mb[0]], b_pM[mb[1]]
        P.op("dve", lambda e: e.tensor_copy(out=qk, in_=m0[:, 0:256]), reads=[bm0], writes=[b_qk])
        P.op("act", lambda e: e.activation(out=vt, in_=m0[:, 256:512], func=AF.Identity), reads=[bm0], writes=[b_vt])
        P.op("dve", lambda e: e.tensor_copy(out=lrT[0:32, :], in_=pM[lb][0:32, 0:128]), reads=[b_pM[lb]], writes=[b_lrT])
        if lat:
            P.op("act", lambda e: e.activation(out=sgt[:, i2, :], in_=m1[:, 0:256], func=AF.Silu), reads=[bm1], writes=[b_sg[i2]])
            P.op("dve", lambda e: e.tensor_copy(out=ut[:, i2, :], in_=m1[:, 256:512]), reads=[bm1], writes=[b_ut[i2]])
            P.dma("sp", lambda e: e.dma_start(out=sgD[n * 128:(n + 1) * 128, :], in_=sgt[:, i2, :]), reads=[b_sg[i2]], writes=[b_sgD])
            P.dma("sp", lambda e: e.dma_start(out=uD[n * 128:(n + 1) * 128, :], in_=ut[:, i2, :]), reads=[b_ut[i2]], writes=[b_uD])

    prev = None
    xload(0)
    for ti in range(len(tiles)):
        hooks = {}
        if prev is not None:
            hooks = {1: prev[0], 5: prev[1], 9: prev[2], 13: prev[3]}
        front(ti, hooks)
        prev = back_stages(ti)
    for s_ in prev:
        s_()
    barrier(); ar.reset()
    Sf = sbt("Sf", [128, 256], F32); Sb_ = sbt("Sb", [128, 256], F32); b_Sf = Buf(); b_Sb = Buf()
    Sfb = R2[:, :].rearrange("p (n v) -> p n v", n=NT); Sbb = R3[:, :].rearrange("p (n v) -> p n v", n=NT)
    Ul = sbt("Ul", [128, 4, 256], F32); b_Ul = [Buf() for _ in range(4)]
    P.op("dve", lambda e: e.memset(Sf[:], 0.0), writes=[b_Sf])
    P.op("pool", lambda e: e.memset(Sb_[:], 0.0), writes=[b_Sb])
    order_f = list(range(NT + NCT))
    order_b = [1, 0] + [NCT + n for n in range(NT - 1, -1, -1)]
    for step in range(NT + NCT):
        for d_, (order, S, bS, Sbf, bR) in enumerate(((order_f, Sf, b_Sf, Sfb, (bR2a, bR2b)), (order_b, Sb_, b_Sb, Sbb, (bR3a, bR3b)))):
            ti = order[step]
            slot = (2 * step + d_) % 4
            P.dma("sp", lambda e, ti=ti, d_=d_, slot=slot: e.dma_start(out=Ul[:, slot, :], in_=UD[ti, :, d_ * 256:(d_ + 1) * 256]),
                  reads=[b_UD], writes=[b_Ul[slot]])
            if ti >= NCT:
                n = ti - NCT
                P.op("act", lambda e, S=S, Sbf=Sbf, n=n: e.activation(out=Sbf[:, n, :], in_=S[:], func=AF.Identity), reads=[bS], writes=list(bR))
            eng = "dve"
            P.op(eng, lambda e, S=S, d_=d_, ti=ti, slot=slot: e.scalar_tensor_tensor(out=S[:], in0=S[:], scalar=dec[:, d_, ti:ti + 1], in1=Ul[:, slot, :],
                                                                                    op0=ALU.mult, op1=ALU.add),
                 reads=[bS, b_dec, b_Ul[slot]], writes=[bS])

    AT = RF[:, 0:16384].rearrange("p (c t) -> p c t", c=2)
    oil = sbt("oil", [128, 2, 256], F32); b_oil = [Buf(), Buf()]
    sgl = sbt("sgl", [128, 2, 256], BF16); b_sgl = [Buf(), Buf()]
    osb = sbt("osb", [128, 256], F32); b_osb = Buf()
    on = sbt("on", [128, 256], F32); b_on = Buf()
    Atok = sbt("Atok", [128, 256], BF16); b_Atok = Buf()
    os2 = sbt("os2", [128, 4], F32); b_os2 = Buf()
    ojunk = sbt("ojunk", [128, 256], F32)
    for n in range(NT):
        i2 = n % 2
        P.dma("sp", lambda e, n=n, i2=i2: e.dma_start(out=oil[:, i2, :], in_=oiD[n * 128:(n + 1) * 128, :]), reads=[b_oiD], writes=[b_oil[i2]])
        P.dma("sp", lambda e, n=n, i2=i2: e.dma_start(out=sgl[:, i2, :], in_=sgD[n * 128:(n + 1) * 128, :]), reads=[b_sgD], writes=[b_sgl[i2]])
        pb = pM[n % 4]; bpb = b_pM[n % 4]
        P.op("pe", lambda e, n=n, pb=pb: e.matmul(pb[:, 0:256], lhsT=qdT[:, 0, n, :], rhs=Sfb[:, n, :], start=True, stop=False),
             reads=[b_qdT, bR2a, bR2b], writes=[bpb])
        P.op("pe", lambda e, n=n, pb=pb: e.matmul(pb[:, 0:256], lhsT=qdT[:, 1, n, :], rhs=Sbb[:, n, :], start=False, stop=True),
             reads=[b_qdT, bR3a, bR3b], writes=[bpb])
        P.op("dve", lambda e, pb=pb, i2=i2: e.tensor_tensor(out=osb[:], in0=pb[:, 0:256], in1=oil[:, i2, :], op=ALU.add), reads=[bpb, b_oil[i2]], writes=[b_osb])
        P.op("act", lambda e: e.activation(out=ojunk[:], in_=osb[:], func=AF.Square, accum_out=os2[:, 0:1]), reads=[b_osb], writes=[b_os2])
        P.op("act", lambda e: e.activation(out=os2[:, 1:2], in_=os2[:, 0:1], func=AF.Sqrt, scale=1.0 / 256.0, bias=EPS), reads=[b_os2], writes=[b_os2])
        P.op("dve", lambda e: e.reciprocal(out=os2[:, 2:3], in_=os2[:, 1:2]), reads=[b_os2], writes=[b_os2])
        P.op("dve", lambda e: e.scalar_tensor_tensor(out=on[:], in0=osb[:], scalar=os2[:, 2:3], in1=ggb[:], op0=ALU.mult, op1=ALU.mult),
             reads=[b_osb, b_os2, b_const], writes=[b_on])
        P.op("pool", lambda e, i2=i2: e.tensor_tensor(out=Atok[:], in0=on[:], in1=sgl[:, i2, :], op=ALU.mult), reads=[b_on, b_sgl[i2]], writes=[b_Atok])
        for c in range(2):
            P.op("pe", lambda e, c=c: e.transpose(out=pT[:, c * 128:(c + 1) * 128], in_=Atok[:, c * 128:(c + 1) * 128], identity=identb[:]),
                 reads=[b_Atok, b_const], writes=[b_pT])
        P.op("act", lambda e, n=n: e.activation(out=AT[:, :, n * 128:(n + 1) * 128], in_=pT[:, 0:256].rearrange("p (c t) -> p c t", c=2), func=AF.Identity),
             reads=[b_pT], writes=[bRFa, bRFb])
    b_srcA = Buf(); b_srcF = Buf(); b_gatA = Buf(); b_gatF = Buf()
    for c in range(2):
        P.dma("sp", lambda e, c=c: e.dma_start(out=srcA[c * 128:(c + 1) * 128, :], in_=AT[:, c, :]), reads=[bRFa, bRFb], writes=[b_srcA])
    if not os.environ.get("NOCC"):
        P.cc(lambda e: e.collective_compute("AllGather", ALU.bypass, replica_groups=[list(range(8))], ins=[srcA], outs=[gatA]),
             reads=[b_srcA], writes=[b_gatA])

    barrier(); ar.reset()
    uT = R1[:, :].rearrange("p (c t) -> p c t", c=2)
    T1c = R2[:, 0:8192].rearrange("p (a k) -> p a k", a=64); T1s = R2[:, 8192:16384].rearrange("p (a k) -> p a k", a=64)
    FT = R3[:, :].rearrange("p (c t) -> p c t", c=2)
    P.dma("sp", lambda e: e.dma_start(out=T1c, in_=T1c_d), writes=[bR2a, bR2b])
    P.dma("sp", lambda e: e.dma_start(out=T1s, in_=T1s_d), writes=[bR2a, bR2b])
    ul = sbt("ul", [128, 2, 256], BF16); b_ul = [Buf(), Buf()]
    for n in range(NT):
        i2 = n % 2
        P.dma("sp", lambda e, n=n, i2=i2: e.dma_start(out=ul[:, i2, :], in_=uD[n * 128:(n + 1) * 128, :]), reads=[b_uD], writes=[b_ul[i2]])
        for c in range(2):
            P.op("pe", lambda e, c=c, i2=i2: e.transpose(out=pT[:, 512 + c * 128:512 + (c + 1) * 128], in_=ul[:, i2, c * 128:(c + 1) * 128], identity=identb[:]),
                 reads=[b_ul[i2], b_const], writes=[b_pT])
        P.op("dve", lambda e, n=n: e.tensor_copy(out=uT[:, :, n * 128:(n + 1) * 128], in_=pT[:, 512:768].rearrange("p (c t) -> p c t", c=2)),
             reads=[b_pT], writes=[bR1a, bR1b, b_qdT])
    Zt = sbt("Zt", [128, 2, 2, 512], BF16); b_Zt = [Buf(), Buf()]
    Gt = sbt("Gt", [128, 2, 512], BF16); b_Gt = [Buf(), Buf()]
    b_GdD = Buf()
    uTa = R1[:, :].rearrange("p (c q a) -> p c a q", c=2, a=64)
    for a in range(64):
        i2 = a % 2
        pz = pM[(2 * a) % 4]; bpz = b_pM[(2 * a) % 4]
        pg = pM[(2 * a + 1) % 4]; bpg = b_pM[(2 * a + 1) % 4]
        for c in range(2):
            P.op("pe", lambda e, c=c, a=a, pz=pz: e.matmul(pz[:, :], lhsT=uTa[:, c, a, :], rhs=Fc[:, c, :], start=(c == 0), stop=(c == 1)),
                 reads=[bR1a, bR1b, b_const], writes=[bpz])
        P.op("dve", lambda e, i2=i2, pz=pz: e.tensor_copy(out=Zt[:, i2, 0, :], in_=pz[:, :]), reads=[bpz], writes=[b_Zt[i2]])
        P.op("act", lambda e, i2=i2, pz=pz: e.activation(out=Zt[:, i2, 1, 0:256], in_=pz[:, 256:512], func=AF.Identity), reads=[bpz], writes=[b_Zt[i2]])
        P.op("act", lambda e, i2=i2, pz=pz: e.activation(out=Zt[:, i2, 1, 256:512], in_=pz[:, 0:256], func=AF.Identity, scale=-1.0), reads=[bpz], writes=[b_Zt[i2]])
        P.op("pe", lambda e, a=a, i2=i2, pg=pg: e.matmul(pg[:, :], lhsT=T1c[:, a, :], rhs=Zt[:, i2, 0, :], start=True, stop=False),
             reads=[bR2a, bR2b, b_Zt[i2]], writes=[bpg])
        P.op("pe", lambda e, a=a, i2=i2, pg=pg: e.matmul(pg[:, :], lhsT=T1s[:, a, :], rhs=Zt[:, i2, 1, :], start=False, stop=True),
             reads=[bR2a, bR2b, b_Zt[i2]], writes=[bpg])
        P.op("dve", lambda e, i2=i2, pg=pg: e.tensor_copy(out=Gt[:, i2, :], in_=pg[:, :]), reads=[bpg], writes=[b_Gt[i2]])
        P.dma("sp", lambda e, a=a, i2=i2: e.dma_start(out=GdD[:, a, :], in_=Gt[:, i2, :]), reads=[b_Gt[i2]], writes=[b_GdD])
    G2 = [RF[0:64, 16384 + i * 8192: 16384 + (i + 1) * 8192].rearrange("p (k n) -> p k n", k=16) for i in range(2)]
    bG2 = [bRFc, bRFd]
    GdT = GdD.rearrange("k a n -> a k n")
    for kb in range(8):
        i2 = kb % 2
        P.dma("sp", lambda e, kb=kb, i2=i2: e.dma_start(out=G2[i2], in_=GdT[:, kb * 16:(kb + 1) * 16, :]), reads=[b_GdD], writes=[bG2[i2]])
        for half in range(2):
            kp0 = kb * 16 + half * 8
            for c in range(2):
                pf = pM[(kb * 4 + half * 2 + c) % 4]; bpf = b_pM[(kb * 4 + half * 2 + c) % 4]
                for q in range(8):
                    kq = half * 8 + q
                    P.op("pe", lambda e, i2=i2, kq=kq, c=c, q=q, pf=pf: e.matmul(pf[:, q * 64:(q + 1) * 64], lhsT=G2[i2][:, kq, c * 128:(c + 1) * 128],
                                                                                rhs=C2S2[:, 0, :], start=True, stop=False),
                         reads=[bG2[i2], b_const], writes=[bpf])
                    P.op("pe", lambda e, i2=i2, kq=kq, c=c, q=q, pf=pf: e.matmul(pf[:, q * 64:(q + 1) * 64], lhsT=G2[i2][:, kq, 256 + c * 128:256 + (c + 1) * 128],
                                                                                rhs=C2S2[:, 1, :], start=False, stop=True),
                         reads=[bG2[i2], b_const], writes=[bpf])
                FTv = R3[:, :].rearrange("p (c ka kp) -> p c kp ka", c=2, ka=64)
                P.op("dve" if c == 0 else "act",
                     (lambda e, c=c, kp0=kp0, pf=pf: e.tensor_copy(out=FTv[:, c, kp0:kp0 + 8, :], in_=pf[:, :].rearrange("p (q k) -> p q k", q=8))) if c == 0 else
                     (lambda e, c=c, kp0=kp0, pf=pf: e.activation(out=FTv[:, c, kp0:kp0 + 8, :], in_=pf[:, :].rearrange("p (q k) -> p q k", q=8), func=AF.Identity)),
                     reads=[bpf], writes=[bR3a, bR3b])
    for c in range(2):
        P.dma("sp", lambda e, c=c: e.dma_start(out=srcF[c * 128:(c + 1) * 128, :], in_=FT[:, c, :]), reads=[bR3a, bR3b], writes=[b_srcF])
    if not os.environ.get("NOCC"):
        P.cc(lambda e: e.collective_compute("AllGather", ALU.bypass, replica_groups=[list(range(8))], ins=[srcF], outs=[gatF]),
             reads=[b_srcF], writes=[b_gatF])

    barrier(); ar.reset()
    hTc = R1[:, 0:8192].rearrange("p (k t) -> p k t", k=16)
    h2T = R1[:, 8192:16384].rearrange("p (k t) -> p k t", k=16)
    AFT = R2[:, 0:8192].rearrange("p (k t) -> p k t", k=16)
    Lt = [R3[:, 0:8192].rearrange("p (k t) -> p k t", k=16), R2[:, 8192:16384].rearrange("p (k t) -> p k t", k=16)]
    mrg = R3[:, 0:8192].rearrange("p (k t) -> p k t", k=16)
    wb = [R2[:, 8192:16384], R3[:, 8192:16384], R1[:, 0:8192], R2[:, 0:8192]]
    b_hTc, b_h2T, b_AFT, b_L0, b_wb0, b_wb1 = bR1a, bR1b, bR2a, bR3a, bR2b, bR3b
    b_Lt = [b_L0, b_wb0]
    b_wb = [b_wb0, b_wb1, b_hTc, b_AFT]
    res = RFf[:, 0:8192].rearrange("p (k t) -> p k t", k=16)
    b_res = bRFa
    mT = RF[:, 28672:32768].rearrange("p (s f t) -> p s f t", s=2, f=4)
    b_mT = [Buf(), Buf()]
    rstd = sbt("rstd", [128, 512], F32); b_rstd = Buf()
    sqb = sbt("sqb", [128, 2, 512], BF16); b_sqb = [Buf(), Buf()]
    onesb = sbt("onesb", [128, 128], BF16); b_onesb = Buf()
    tmpf = sbt("tmpf", [128, 2, 512], F32); b_tmpf = [Buf(), Buf()]
    xnc = sbt("xnc", [128, 2048], BF16); b_xnc = Buf()
    P.op("pool", lambda e: e.memset(onesb[:], 1.0), writes=[b_onesb])
    out_dmas = []
    gatAv = gatA.rearrange("(r c p) t -> p r c t", r=8, c=2)
    gatFv = gatF.rearrange("(r c p) t -> p r c t", r=8, c=2)
    wGv = wG_d.rearrange("(k p) n -> p k n", p=128); wfov = wfo_d.rearrange("(k p) n -> p k n", p=128)
    wgov = wgo_d.rearrange("(k p) n -> p k n", p=128); wov = wo_d.rearrange("(k p) n -> p k n", p=128)
    w1v = w1_d.rearrange("(k p) n -> p k n", p=128); w2v = w2_d.rearrange("(k p) n -> p k n", p=128)

    def rms_fm():
        pq = pS[1]
        for k in range(16):
            j = k % 2
            P.op("act", lambda e, k=k, j=j: e.activation(out=sqb[:, j, :], in_=res[:, k, :], func=AF.Square), reads=[b_res], writes=[b_sqb[j]])
            P.op("pe", lambda e, k=k, j=j: e.matmul(pq[:, :], lhsT=onesb[:], rhs=sqb[:, j, :], start=(k == 0), stop=(k == 15)),
                 reads=[b_sqb[j], b_onesb], writes=[b_pS[1]])
        P.op("act", lambda e: e.activation(out=rstd[:], in_=pq[:, :], func=AF.Sqrt, scale=1.0 / D, bias=EPS), reads=[b_pS[1]], writes=[b_rstd])
        P.op("dve", lambda e: e.reciprocal(out=rstd[:], in_=rstd[:]), reads=[b_rstd], writes=[b_rstd])

    for pp in range(4):
        for tl in range(4):
            tg = pp * 4 + tl
            xb = xt[tg % NXB]; bxb = b_xt[tg % NXB]; i2 = tg % 2
            P.dma("sp", lambda e, xb=xb, tg=tg: e.dma_start(out=xb, in_=x_own[tg * 128:(tg + 1) * 128, :]), writes=[bxb])
            P.dma("sp", lambda e, tg=tg, i2=i2: e.dma_start(out=prt[i2], in_=PRo_d[tg]), writes=[b_prt[i2]])
            P.op("dve", lambda e, xb=xb, i2=i2: e.tensor_tensor(out=xb[:, 0:1024], in0=xb[:, 0:1024], in1=prt[i2], op=ALU.add), reads=[b_prt[i2]], writes=[bxb])
            P.op("pool", lambda e, xb=xb: e.tensor_tensor(out=xb[:, 1024:2048], in0=xb[:, 1024:2048], in1=PC[:], op=ALU.add), reads=[b_const], writes=[bxb])
            P.op("act", lambda e, xb=xb: e.activation(out=xnc[:], in_=xb, func=AF.Square, accum_out=ssq[:, 0:1]), reads=[bxb], writes=[b_xnc, b_ssq])
            P.op("act", lambda e: e.activation(out=ssq[:, 1:2], in_=ssq[:, 0:1], func=AF.Sqrt, scale=1.0 / D, bias=EPS), reads=[b_ssq], writes=[b_ssq])
            P.op("dve", lambda e: e.reciprocal(out=ssq[:, 2:3], in_=ssq[:, 1:2]), reads=[b_ssq], writes=[b_ssq])
            P.op("act", lambda e, xb=xb: e.activation(out=xnc[:], in_=xb, func=AF.Copy, scale=ssq[:, 2:3]), reads=[bxb, b_ssq], writes=[b_xnc])
            for k in range(16):
                P.op("pe", lambda e, k=k: e.transpose(out=pT[:, k * 128:(k + 1) * 128], in_=xnc[:, k * 128:(k + 1) * 128], identity=identb[:]),
                     reads=[b_xnc, b_const], writes=[b_pT])
            for k in range(16):
                P.op("dve", lambda e, k=k, tl=tl: e.tensor_scalar(out=hTc[:, k, tl * 128:(tl + 1) * 128], in0=pT[:, k * 128:(k + 1) * 128],
                                                                  scalar1=ps[:, 0, k:k + 1], scalar2=ps[:, 1, k:k + 1], op0=ALU.mult, op1=ALU.add),
                     reads=[b_pT, b_ps], writes=[b_hTc])
            for g in range(4):
                for q in range(4):
                    k = 4 * g + q
                    P.op("pe", lambda e, k=k, q=q, g=g, xb=xb: e.transpose(out=pM[g][:, q * 128:(q + 1) * 128], in_=xb[:, k * 128:(k + 1) * 128], identity=identf[:]),
                         reads=[bxb, b_const], writes=[b_pM[g]])
                P.op("act", lambda e, g=g, tl=tl: e.activation(out=res[:, 4 * g:4 * g + 4, tl * 128:(tl + 1) * 128],
                                                              in_=pM[g][:, :].rearrange("p (q t) -> p q t", q=4), func=AF.Identity),
                     reads=[b_pM[g]], writes=[b_res])
        for cand in range(8):
            bb, ii = cand // 4, cand % 4
            tok0 = ii * 2048 + pp * 512
            L = Lt[cand % 2]; bL = b_Lt[cand % 2]
            for jp in range(4):
                r = 4 * bb + jp
                P.dma("sp", lambda e, L=L, jp=jp, r=r, tok0=tok0: e.dma_start(out=L[:, 2 * jp:2 * jp + 2, :], in_=gatAv[:, r, :, tok0:tok0 + 512]),
                      reads=[b_gatA], writes=[bL])
                P.dma("sp", lambda e, L=L, jp=jp, r=r, tok0=tok0: e.dma_start(out=L[:, 8 + 2 * jp:8 + 2 * jp + 2, :], in_=gatFv[:, r, :, tok0:tok0 + 512]),
                      reads=[b_gatF], writes=[bL])
            if cand == 0:
                P.op("dve", lambda e, L=L, cand=cand: e.tensor_scalar(out=AFT, in0=L, scalar1=sel[:, cand:cand + 1], scalar2=None, op0=ALU.mult),
                     reads=[bL, b_const], writes=[b_AFT])
            else:
                P.op("dve", lambda e, L=L, cand=cand: e.scalar_tensor_tensor(out=AFT, in0=L, scalar=sel[:, cand:cand + 1], in1=AFT, op0=ALU.mult, op1=ALU.add),
                     reads=[bL, b_const], writes=[b_AFT])
        for m in range(16):
            i = m % 2
            wbv = wb[i][:, 0:6144].rearrange("p (k n) -> p k n", n=128)
            cs = slice(m * 128, (m + 1) * 128)
            P.dma("pool", lambda e, wbv=wbv, cs=cs: e.dma_start(out=wbv[:, 0:16, :], in_=wGv[:, :, cs]), writes=[b_wb[i]])
            P.dma("pool", lambda e, wbv=wbv, cs=cs: e.dma_start(out=wbv[:, 16:24, :], in_=wfov[:, :, cs]), writes=[b_wb[i]])
            P.dma("pool", lambda e, wbv=wbv, m=m: e.dma_start(out=wbv[:, 24:40, :], in_=wGv[:, :, 2048 + m * 128:2048 + (m + 1) * 128]), writes=[b_wb[i]])
            P.dma("pool", lambda e, wbv=wbv, cs=cs: e.dma_start(out=wbv[:, 40:48, :], in_=wgov[:, :, cs]), writes=[b_wb[i]])
            for k in range(16):
                P.op("pe", lambda e, k=k, wbv=wbv: e.matmul(pM[0][:, :], lhsT=wbv[:, k, :], rhs=hTc[:, k, :], start=(k == 0), stop=(k == 15)),
                     reads=[b_wb[i], b_hTc], writes=[b_pM[0]])
            for k in range(8):
                P.op("pe", lambda e, k=k, wbv=wbv: e.matmul(pM[1][:, :], lhsT=wbv[:, 16 + k, :], rhs=AFT[:, 8 + k, :], start=(k == 0), stop=(k == 7)),
                     reads=[b_wb[i], b_AFT], writes=[b_pM[1]])
            for k in range(16):
                P.op("pe", lambda e, k=k, wbv=wbv: e.matmul(pM[2][:, :], lhsT=wbv[:, 24 + k, :], rhs=hTc[:, k, :], start=(k == 0), stop=(k == 15)),
                     reads=[b_wb[i], b_hTc], writes=[b_pM[2]])
            for k in range(8):
                P.op("pe", lambda e, k=k, wbv=wbv: e.matmul(pM[3][:, :], lhsT=wbv[:, 40 + k, :], rhs=AFT[:, k, :], start=(k == 0), stop=(k == 7)),
                     reads=[b_wb[i], b_AFT], writes=[b_pM[3]])
            P.op("act", lambda e: e.activation(out=tmpf[:, 0, :], in_=pM[0][:, :], func=AF.Sigmoid), reads=[b_pM[0]], writes=[b_tmpf[0]])
            P.op("dve", lambda e: e.tensor_tensor(out=tmpf[:, 0, :], in0=tmpf[:, 0, :], in1=pM[1][:, :], op=ALU.mult), reads=[b_pM[1]], writes=[b_tmpf[0]])
            P.op("act", lambda e: e.activation(out=tmpf[:, 1, :], in_=pM[2][:, :], func=AF.Sigmoid), reads=[b_pM[2]], writes=[b_tmpf[1]])
            P.op("dve", lambda e: e.tensor_tensor(out=tmpf[:, 1, :], in0=tmpf[:, 1, :], in1=pM[3][:, :], op=ALU.mult), reads=[b_pM[3]], writes=[b_tmpf[1]])
            P.op("pool", lambda e, m=m: e.tensor_tensor(out=mrg[:, m, :], in0=tmpf[:, 0, :], in1=tmpf[:, 1, :], op=ALU.add),
                 reads=[b_tmpf[0], b_tmpf[1]], writes=[b_L0])
        for m in range(16):
            i = m % 2
            wbv = wb[1][:, i * 2048:(i + 1) * 2048].rearrange("p (k n) -> p k n", n=128)
            bw = b_wb1
            P.dma("pool", lambda e, wbv=wbv, m=m: e.dma_start(out=wbv, in_=wov[:, :, m * 128:(m + 1) * 128]), writes=[bw])
            pb = pM[m % 4]; bpb = b_pM[m % 4]
            for k in range(16):
                P.op("pe", lambda e, k=k, wbv=wbv, pb=pb: e.matmul(pb[:, :], lhsT=wbv[:, k, :], rhs=mrg[:, k, :], start=(k == 0), stop=(k == 15)),
                     reads=[bw, b_L0], writes=[bpb])
            P.op("dve", lambda e, m=m, pb=pb: e.scalar_tensor_tensor(out=res[:, m, :], in0=pb[:, :], scalar=ps[:, 4, m:m + 1], in1=res[:, m, :], op0=ALU.mult, op1=ALU.add),
                 reads=[bpb, b_ps], writes=[b_res])
        rms_fm()
        for k in range(16):
            j = k % 2
            P.op("dve", lambda e, k=k, j=j: e.scalar_tensor_tensor(out=tmpf[:, j, :], in0=res[:, k, :], scalar=ps[:, 5, k:k + 1], in1=rstd[:], op0=ALU.mult, op1=ALU.mult),
                 reads=[b_res, b_rstd, b_ps], writes=[b_tmpf[j]])
            P.op("act", lambda e, k=k, j=j: e.activation(out=h2T[:, k, :], in_=tmpf[:, j, :], func=AF.Identity, bias=ps[:, 6, k:k + 1]),
                 reads=[b_tmpf[j], b_ps], writes=[b_h2T])
        for s_ in range(16):
            i = s_ % 2
            w1b = wb[i].rearrange("p (k n) -> p k n", k=16)
            w2b = wb[2 + i].rearrange("p (f n) -> p f n", f=4)
            P.dma("pool", lambda e, w1b=w1b, s_=s_: e.dma_start(out=w1b, in_=w1v[:, :, s_ * 512:(s_ + 1) * 512]), writes=[b_wb[i]])
            P.dma("pool", lambda e, w2b=w2b, s_=s_: e.dma_start(out=w2b, in_=w2v[:, 4 * s_:4 * s_ + 4, :]), writes=[b_wb[2 + i]])
            for f in range(4):
                pb = pM[f]; bpb = b_pM[f]
                for k in range(16):
                    P.op("pe", lambda e, k=k, f=f, w1b=w1b, pb=pb: e.matmul(pb[:, :], lhsT=w1b[:, k, f * 128:(f + 1) * 128], rhs=h2T[:, k, :], start=(k == 0), stop=(k == 15)),
                         reads=[b_wb[i], b_h2T], writes=[bpb])
                j = f % 2
                P.op("act", lambda e, j=j, pb=pb: e.activation(out=tmpf[:, j, :], in_=pb[:, :], func=AF.Relu), reads=[bpb], writes=[b_tmpf[j]])
                P.op("pool", lambda e, j=j, f=f, i=i: e.tensor_tensor(out=mT[:, i, f, :], in0=tmpf[:, j, :], in1=tmpf[:, j, :], op=ALU.mult),
                     reads=[b_tmpf[j]], writes=[b_mT[i]])
            for m in range(16):
                pb = pS[m % 2]; bpb = b_pS[m % 2]
                for f in range(4):
                    P.op("pe", lambda e, f=f, m=m, i=i, w2b=w2b, pb=pb: e.matmul(pb[:, :], lhsT=w2b[:, f, m * 128:(m + 1) * 128], rhs=mT[:, i, f, :], start=(f == 0), stop=(f == 3)),
                         reads=[b_wb[2 + i], b_mT[i]], writes=[bpb])
                P.op("dve", lambda e, m=m, pb=pb: e.scalar_tensor_tensor(out=res[:, m, :], in0=pb[:, :], scalar=ps[:, 7, m:m + 1], in1=res[:, m, :], op0=ALU.mult, op1=ALU.add),
                     reads=[bpb, b_ps], writes=[b_res])
        rms_fm()
        for k in range(16):
            P.op("dve", lambda e, k=k: e.scalar_tensor_tensor(out=res[:, k, :], in0=res[:, k, :], scalar=ps[:, 8, k:k + 1], in1=rstd[:], op0=ALU.mult, op1=ALU.mult),
                 reads=[b_rstd, b_ps], writes=[b_res])
        for tl in range(4):
            tg = pp * 4 + tl
            ob = xt[tg % NXB]; bob = b_xt[tg % NXB]
            for g in range(4):
                for q in range(4):
                    k = 4 * g + q
                    P.op("pe", lambda e, k=k, q=q, g=g, tl=tl: e.transpose(out=pM[g][:, q * 128:(q + 1) * 128], in_=res[:, k, tl * 128:(tl + 1) * 128], identity=identf[:]),
                         reads=[b_res, b_const], writes=[b_pM[g]])
                P.op("act" if g % 2 else "dve",
                     (lambda e, g=g, ob=ob: e.activation(out=ob[:, g * 512:(g + 1) * 512], in_=pM[g][:, :], func=AF.Identity)) if g % 2 else
                     (lambda e, g=g, ob=ob: e.tensor_copy(out=ob[:, g * 512:(g + 1) * 512], in_=pM[g][:, :])),
                     reads=[b_pM[g]], writes=[bob])
            out_dmas.append(P.dma("sp", lambda e, tg=tg, ob=ob: e.dma_start(out=out_d[tg * 128:(tg + 1) * 128, :], in_=ob), reads=[bob]))
    P.emit(final_waits=out_dmas)
    st.close()


_NC = None


def kernel(x, c, ctx, c_ctx, w_mod, b_mod, norm1_g, norm2_g, w_in, w_lr_f, b_lr_f, w_lr_b, b_lr_b, gla_norm_g,
           w_fourier_out, w_gla_out, w_out, w_mlp_in, w_mlp_out, final_norm_g):
    global _NC
    f32 = lambda a: np.ascontiguousarray(np.asarray(a, dtype=np.float32))
    x = f32(x); ctx = f32(ctx); w_in = f32(w_in)[0]
    cst = _consts()
    if _NC is None:
        _NC = build_nc()
    nc = _NC
    gv = np.stack([f32(norm1_g)[0], f32(norm2_g)[0], f32(final_norm_g)], axis=0)
    gvecs = np.ascontiguousarray(gv.reshape(3, 16, 128).transpose(2, 0, 1))
    off_q, off_k, off_v, off_g, off_lrf, off_lrb, off_u, off_gate = 0, 512, 1024, 2048, 3072, 3088, 3104, 4128
    shared = {
        "w_mod": f32(w_mod)[0], "b_mod2": np.ascontiguousarray(np.repeat(f32(b_mod), 2, axis=0)),
        "gvecs": gvecs, "ggrow": f32(gla_norm_g).reshape(1, 256),
        "wLR": np.ascontiguousarray(w_in[:, off_lrf:off_lrf + 32]), "wG": np.ascontiguousarray(w_in[:, off_gate:off_gate + 4096]),
        "w_fo": f32(w_fourier_out)[0], "w_go": f32(w_gla_out)[0], "w_o": f32(w_out)[0], "w1": f32(w_mlp_in)[0], "w2": f32(w_mlp_out)[0],
        "PR": cst["PR"], "PC": cst["PC"], "Fc": cst["Fc"], "T1c": cst["T1c"], "T1s": cst["T1s"], "C2S2": cst["C2S2"],
        "tri": cst["tri"], "mask2": cst["mask2"], "identb": cst["identb"], "identf": cst["identf"],
    }
    wlf = f32(w_lr_f)[0]; wlb = f32(w_lr_b)[0]; blf = f32(b_lr_f)[0]; blb = f32(b_lr_b)[0]
    in_maps = []
    for core in range(8):
        b, j = core // 4, core % 4
        wA = np.concatenate([w_in[:, off_q + 128 * j: off_q + 128 * (j + 1)], w_in[:, off_k + 128 * j: off_k + 128 * (j + 1)],
                             w_in[:, off_v + 256 * j: off_v + 256 * (j + 1)], w_in[:, off_g + 256 * j: off_g + 256 * (j + 1)],
                             w_in[:, off_u + 256 * j: off_u + 256 * (j + 1)]], axis=1)
        wlr = np.zeros((33, 256), np.float32)
        wlr[0:16, 0:128] = wlf[:, 128 * j:128 * (j + 1)]
        wlr[16:32, 128:256] = wlb[:, 128 * j:128 * (j + 1)]
        wlr[32, 0:128] = blf[128 * j:128 * (j + 1)]
        wlr[32, 128:256] = blb[128 * j:128 * (j + 1)]
        sel = np.zeros((128, 8), np.float32); sel[:, core] = 1.0
        m = dict(shared)
        m.update({
            "x_b": x[b], "ctx_b": ctx[b], "x_own": np.ascontiguousarray(x[b, 2048 * j:2048 * (j + 1)]),
            "cvecT": np.ascontiguousarray(np.stack([f32(c)[b], f32(c_ctx)], axis=1)),
            "wA": np.ascontiguousarray(wA), "wlr": wlr, "PRown": np.ascontiguousarray(cst["PR"][16 * j:16 * (j + 1)]), "sel": sel,
        })
        in_maps.append(m)
    res = run_bass_kernel_spmd(nc, in_maps, core_ids=list(range(8)))
    out = np.empty((2, T, D), np.float32)
    for core in range(8):
        b, j = core // 4, core % 4
        out[b, 2048 * j:2048 * (j + 1)] = np.asarray(res.results[core]["out"], dtype=np.float32)
    return out
```
